# Optimizing a Trainium2 kernel written in Bass

```python
import math
import jax, jax.numpy as jnp
from jax import lax
import numpy as np

D_MODEL = 2048
BATCH = 4
SEQ = 4096
DEPTH = 1

MEM_LEN = 256

POOL_WIDTH = D_MODEL // 2
POOL_WINDOWS = (2, 4, 8, 16)
POOL_GROUPS = len(POOL_WINDOWS)
POOL_GROUP_DIM = POOL_WIDTH // POOL_GROUPS

SGU_WIDTH = D_MODEL // 2
SGU_CHUNK = 128
SGU_HEADS = 8
SGU_HEAD_DIM = SGU_WIDTH // SGU_HEADS

XATTN_HEADS = 4
XATTN_HEAD_DIM = D_MODEL // 8
XATTN_WIDTH = XATTN_HEADS * XATTN_HEAD_DIM

BRANCH_WIDTHS = (POOL_WIDTH, SGU_WIDTH, XATTN_WIDTH)
MIX_WIDTH = POOL_WIDTH + SGU_WIDTH + XATTN_WIDTH
IN_SPLITS = (POOL_WIDTH, POOL_WIDTH, SGU_WIDTH, SGU_WIDTH, SGU_WIDTH, XATTN_WIDTH, XATTN_WIDTH)
IN_WIDTH = sum(IN_SPLITS)
EPS = 1e-6

kernel_name = "hybrid_pool_sgu_memxattn_layer"


def rmsnorm(x, g):
    xf = x.astype(jnp.float32)
    y = xf * lax.rsqrt(jnp.mean(xf * xf, axis=-1, keepdims=True) + EPS)
    return (y * g.astype(jnp.float32)).astype(x.dtype)


def layernorm(x, g, b):
    xf = x.astype(jnp.float32)
    mu = jnp.mean(xf, axis=-1, keepdims=True)
    xc = xf - mu
    y = xc * lax.rsqrt(jnp.mean(xc * xc, axis=-1, keepdims=True) + EPS)
    return (y * g.astype(jnp.float32) + b.astype(jnp.float32)).astype(x.dtype)


def split_cols(a, sizes):
    idx = list(np.cumsum(sizes)[:-1])
    return jnp.split(a, idx, axis=-1)


def pool_mixer(xa, w_pool, scale):
    B, S, _ = xa.shape
    xg = xa.reshape(B, S, POOL_GROUPS, POOL_GROUP_DIM).astype(jnp.float32)
    csum = jnp.cumsum(xg, axis=1)
    t = jnp.arange(1, S + 1, dtype=jnp.float32)
    outs = []
    for g, w in enumerate(POOL_WINDOWS):
        cg = csum[:, :, g]
        lower = jnp.pad(cg[:, :S - w], ((0, 0), (w, 0), (0, 0)))
        count = jnp.minimum(t, float(w))[None, :, None]
        outs.append((cg - lower) / count - xg[:, :, g])
    d = jnp.stack(outs, axis=2).astype(xa.dtype)
    y = jnp.einsum('bsgc,gcd->bsgd', d, w_pool)
    return y.reshape(B, S, POOL_WIDTH) * scale


def spatial_gating(u, v, ln_g, ln_b, w_s, b_s):
    B, S, _ = v.shape
    n_chunks = S // SGU_CHUNK
    vn = layernorm(v, ln_g, ln_b)
    vc = vn.reshape(B, n_chunks, SGU_CHUNK, SGU_HEADS, SGU_HEAD_DIM)
    causal = jnp.tril(jnp.ones((SGU_CHUNK, SGU_CHUNK), dtype=bool))
    w = jnp.where(causal[None], w_s, jnp.zeros_like(w_s))
    z = jnp.einsum('hts,bnshd->bnthd', w, vc) + jnp.transpose(b_s)[None, None, :, :, None]
    return u * z.reshape(B, S, SGU_WIDTH)


def memory_cross_attention(q, k, v):
    B, S, _ = q.shape
    M = k.shape[1]
    qh = q.reshape(B, S, XATTN_HEADS, XATTN_HEAD_DIM)
    kh = k.reshape(B, M, XATTN_HEADS, XATTN_HEAD_DIM)
    vh = v.reshape(B, M, XATTN_HEADS, XATTN_HEAD_DIM)
    s = jnp.einsum('bshd,bmhd->bhsm', qh, kh).astype(jnp.float32) * (1.0 / math.sqrt(XATTN_HEAD_DIM))
    p = jax.nn.softmax(s, axis=-1).astype(vh.dtype)
    o = jnp.einsum('bhsm,bmhd->bshd', p, vh)
    return o.reshape(B, S, XATTN_WIDTH)


def setup_inputs(seed: int = 0) -> dict:
    key = jax.random.key(seed)
    ks = jax.random.split(key, 16)
    f32 = jnp.float32
    nrm = lambda k, shape, s: jax.random.normal(k, shape, f32) * s
    return {
        "x": nrm(ks[0], (BATCH, SEQ, D_MODEL), 1.0),
        "mem": nrm(ks[1], (BATCH, MEM_LEN, D_MODEL), 1.0),
        "norm_pre": 1.0 + nrm(ks[2], (DEPTH, D_MODEL), 0.05),
        "w_in": nrm(ks[3], (DEPTH, D_MODEL, IN_WIDTH), D_MODEL ** -0.5),
        "pool_w": nrm(ks[4], (DEPTH, POOL_GROUPS, POOL_GROUP_DIM, POOL_GROUP_DIM), POOL_GROUP_DIM ** -0.5),
        "pool_scale": 1.0 + nrm(ks[5], (DEPTH, POOL_WIDTH), 0.1),
        "sgu_ln_g": 1.0 + nrm(ks[6], (DEPTH, SGU_WIDTH), 0.05),
        "sgu_ln_b": nrm(ks[7], (DEPTH, SGU_WIDTH), 0.02),
        "sgu_w": nrm(ks[8], (DEPTH, SGU_HEADS, SGU_CHUNK, SGU_CHUNK), SGU_CHUNK ** -0.5),
        "sgu_b": 1.0 + nrm(ks[9], (DEPTH, SGU_HEADS, SGU_CHUNK), 0.1),
        "mem_norm": 1.0 + nrm(ks[10], (D_MODEL,), 0.05),
        "w_kv": nrm(ks[11], (DEPTH, D_MODEL, 2 * XATTN_WIDTH), D_MODEL ** -0.5),
        "branch_norm": 1.0 + nrm(ks[12], (DEPTH, MIX_WIDTH), 0.05),
        "w_out": nrm(ks[13], (DEPTH, MIX_WIDTH, D_MODEL), MIX_WIDTH ** -0.5),
        "norm_post": 1.0 + nrm(ks[14], (DEPTH, D_MODEL), 0.05),
    }


def reference(x, mem, norm_pre, w_in, pool_w, pool_scale, sgu_ln_g, sgu_ln_b, sgu_w, sgu_b,
              mem_norm, w_kv, branch_norm, w_out, norm_post):
    mem_n = rmsnorm(mem, mem_norm)
    for l in range(DEPTH):
        h = rmsnorm(x, norm_pre[l])
        proj = jnp.einsum('bsd,de->bse', h, w_in[l])
        xa, ga, u, vb, gb, q, gc = split_cols(proj, IN_SPLITS)
        k_m, v_m = split_cols(jnp.einsum('bmd,de->bme', mem_n, w_kv[l]), (XATTN_WIDTH, XATTN_WIDTH))

        ya = pool_mixer(xa, pool_w[l], pool_scale[l]) * jax.nn.silu(ga)
        yb = spatial_gating(u, vb, sgu_ln_g[l], sgu_ln_b[l], sgu_w[l], sgu_b[l]) * jax.nn.silu(gb)
        yc = memory_cross_attention(q, k_m, v_m) * jax.nn.silu(gc)

        g_a, g_b, g_c = split_cols(branch_norm[l], BRANCH_WIDTHS)
        y = jnp.concatenate([rmsnorm(ya, g_a), rmsnorm(yb, g_b), rmsnorm(yc, g_c)], axis=-1)
        out = jnp.einsum('bse,ed->bsd', y, w_out[l])
        x = x + rmsnorm(out, norm_post[l])
    return x
```

```python
import numpy as np
import ml_dtypes
from contextlib import ExitStack
import concourse.bass as bass
import concourse.mybir as mybir
from concourse.bass_utils import run_bass_kernel_spmd

F32 = mybir.dt.float32
BF16 = mybir.dt.bfloat16
AF = mybir.ActivationFunctionType
ALU = mybir.AluOpType
AX = mybir.AxisListType

D = 2048
TC = 2048
TP = 1024
NPASS = TC // TP
NTT = TP // 128
EPS = 1e-6
POOL_WINDOWS = (2, 4, 8, 16)

C_XA, C_GA, C_U, C_V, C_GB, C_Q, C_GC = 0, 1024, 2048, 3072, 4096, 5120, 6144


class Buf:
    __slots__ = ("name", "wtok", "rtoks")

    def __init__(self, name):
        self.name = name
        self.wtok = None
        self.rtoks = {}


class Prog:
    ENG = ("pe", "act", "dve", "pool", "sp")

    def __init__(self):
        self.lists = {e: [] for e in self.ENG}
        self.count = {}
        self.is_dma_sem = set()

    def _newtok(self, sem, inc):
        self.count[sem] = self.count.get(sem, 0) + inc
        return (sem, self.count[sem])

    def op(self, eng, fn, reads=(), writes=(), waits=(), dma_sem=None, signal=True, first_waits_only=False):
        w = {}

        def add(t):
            if t is None:
                return
            s, v = t
            if w.get(s, 0) < v:
                w[s] = v
        for t in waits:
            add(t)
        for b in reads:
            add(b.wtok)
        for b in writes:
            add(b.wtok)
            for s, v in b.rtoks.items():
                add((s, v))
        tok = None
        if dma_sem is not None:
            tok = self._newtok(dma_sem, 16)
            self.is_dma_sem.add(dma_sem)
        elif signal:
            tok = self._newtok("e_" + eng, 1)
        self.lists[eng].append((fn, sorted(w.items()), tok))
        if tok is not None:
            for b in reads:
                if b.rtoks.get(tok[0], 0) < tok[1]:
                    b.rtoks[tok[0]] = tok[1]
            for b in writes:
                b.wtok = tok
                b.rtoks = {}
        return tok

    def group(self, eng, fns, reads=(), writes=(), waits=()):
        n = len(fns)
        if n == 1:
            return self.op(eng, fns[0], reads, writes, waits=waits)
        w = {}

        def add(t):
            if t is None:
                return
            s, v = t
            if w.get(s, 0) < v:
                w[s] = v
        for t in waits:
            add(t)
        for b in reads:
            add(b.wtok)
        for b in writes:
            add(b.wtok)
            for s, v in b.rtoks.items():
                add((s, v))
        self.lists[eng].append((fns[0], sorted(w.items()), None))
        for f in fns[1:-1]:
            self.lists[eng].append((f, [], None))
        tok = self._newtok("e_" + eng, 1)
        self.lists[eng].append((fns[-1], [], tok))
        for b in reads:
            if b.rtoks.get(tok[0], 0) < tok[1]:
                b.rtoks[tok[0]] = tok[1]
        for b in writes:
            b.wtok = tok
            b.rtoks = {}
        return tok

    def barrier(self, exclude=()):
        toks = sorted(self.count.items())
        for e in self.ENG:
            if e in exclude:
                continue
            self.lists[e].append((None, list(toks), None))

    def snapshot(self, sems=None):
        return [(s, v) for s, v in sorted(self.count.items()) if sems is None or s in sems]

    def emit(self, nc, block, sems):
        engmap = {"pe": block.tensor, "act": block.scalar, "dve": block.vector,
                  "pool": block.gpsimd, "sp": block.sync}
        for e in self.ENG:
            lst = self.lists[e]
            own = "e_" + e

            def body(eng, lst=lst, own=own, e=e):
                seen = {}
                for fn, waits, tok in lst:
                    for s, v in waits:
                        if s == own and e == "pe":
                            continue
                        if seen.get(s, 0) >= v:
                            continue
                        eng.wait_ge(sems[s], v)
                        seen[s] = v
                    if fn is None:
                        continue
                    ins = fn(eng)
                    if tok is not None:
                        ins.then_inc(sems[tok[0]], 16 if tok[0] in self.is_dma_sem else 1)
            engmap[e](body)


def _band_constants():
    t = np.arange(128)[None, :]
    s = np.arange(128)[:, None]
    cur = np.zeros((4, 128, 128), np.float64)
    prev = np.zeros((4, 128, 128), np.float64)
    cur0 = np.zeros((4, 128, 128), np.float64)
    for g, w in enumerate(POOL_WINDOWS):
        inwin = (s <= t) & (s > t - w)
        cur[g] = np.where(inwin, 1.0 / w, 0.0) - (s == t)
        prev[g] = np.where((s - 128) > (t - w), 1.0 / w, 0.0)
        cnt = np.minimum(t + 1, w).astype(np.float64)
        cur0[g] = np.where(inwin, 1.0 / cnt, 0.0) - (s == t)
    return cur, prev, cur0


def _bf16_hi_lo(a):
    hi = a.astype(np.float32).astype(ml_dtypes.bfloat16)
    lo = (a - hi.astype(np.float64)).astype(np.float32).astype(ml_dtypes.bfloat16)
    return hi, lo


def build_program(debug=False):
    nc = bass.Bass("TRN2", target_bir_lowering=False)
    P = Prog()

    def din(name, shape, dt=F32):
        return nc.dram_tensor(name, list(shape), dt, kind="ExternalInput").ap()

    x_d = din("x", [TC, D])
    xh_d = din("xh", [128, D])
    mem_d = din("mem", [256, D])
    w_in_d = din("w_in", [D, 7168])
    w_kv_d = din("w_kv", [D, 2048])
    w_out_d = din("w_out", [3072, D])
    norm_pre_d = din("norm_pre", [D])
    mem_norm_d = din("mem_norm", [D])
    norm_post_d = din("norm_post", [D])
    pool_w_d = din("pool_w", [4, 256, 256])
    pool_scale_d = din("pool_scale", [1024])
    ln_g_d = din("sgu_ln_g", [1024])
    ln_b_d = din("sgu_ln_b", [1024])
    sgu_w_d = din("sgu_w", [8, 128, 128])
    sgu_b_d = din("sgu_b", [1024])
    bn_d = din("branch_norm", [3072])
    ident_d = din("c_ident", [128, 128], BF16)
    tril_d = din("c_tril", [128, 128])
    mcur_d = din("c_mcur", [128, 4, 128], BF16)
    mprev_d = din("c_mprev", [128, 4, 128], BF16)
    mc0h_d = din("c_mcur0h", [128, 4, 128], BF16)
    mc0l_d = din("c_mcur0l", [128, 4, 128], BF16)
    mp0_d = din("c_mprev0", [128, 4, 128], BF16)
    out_d = nc.dram_tensor("out", [TC, D], F32, kind="ExternalOutput").ap()

    es = ExitStack()

    def sb(name, shape, dt):
        return es.enter_context(nc.sbuf_tensor(name, list(shape), dt))

    ident = sb("ident", [128, 128], BF16)
    ones_bf = sb("ones_bf", [128, 8], BF16)
    trilm = sb("trilm", [128, 128], F32)
    mcur = sb("mcur", [128, 4, 128], BF16)
    mprev = sb("mprev", [128, 4, 128], BF16)
    mc0h = sb("mc0h", [128, 4, 128], BF16)
    mc0l = sb("mc0l", [128, 4, 128], BF16)
    mp0 = sb("mp0", [128, 4, 128], BF16)
    wsT = sb("wsT", [128, 8, 128], BF16)
    poolw = sb("poolw", [128, 4, 2, 256], BF16)
    pscale = sb("pscale", [128, 8], F32)
    gbr = sb("gbr", [128, 24], F32)
    kT = sb("kT", [128, 8, 256], BF16)
    vmem = sb("vmem", [128, 2, 1024], BF16)
    xa_save = sb("xa_save", [128, 1024], BF16)
    stat = sb("stat", [128, 640], F32)
    junk = sb("junk", [128, 1024], BF16)
    memnT_t = sb("memnT", [128, 16, 256], BF16)
    smx = sb("smx", [128, 1536], F32)

    hT = sb("hT", [128, 16 * TP], BF16)
    yT = sb("yT", [128, 24 * TP], BF16)
    vnr = sb("vnr", [128, 8192], BF16)
    wpool = sb("wpool", [128, 6 * 4096], BF16)
    stg = sb("stg", [128, 5632], F32)

    hT3 = hT[:].rearrange("p (k t) -> p k t", k=16)
    yT3 = yT[:].rearrange("p (c t) -> p c t", c=24)
    vn3 = vnr[:].rearrange("p (a f) -> p a f", a=8)
    hT_halo = yT[:, 22528:24576].rearrange("p (k t) -> p k t", k=16)

    def wslot(i):
        return wpool[:, i * 4096:(i + 1) * 4096].rearrange("p (k c) -> p k c", k=16)

    def woslot(i):
        return wpool[:, i * 6144:(i + 1) * 6144].rearrange("p (k c) -> p k c", k=12)

    yTf = yT[:].bitcast(F32)
    xt_v = [yTf[:, 0:2048], yTf[:, 2048:4096]]
    gb_v = yTf[:, 4096:6144]
    xs_v = [yT[:, 12288:14336], yT[:, 14336:16384]]
    lng_v = yTf[:, 8192:9216]
    lnb_v = yTf[:, 9216:10240]
    bsb_v = yTf[:, 10240:11264].rearrange("p (h t) -> p h t", h=8)
    stg_tmp = [stg[:, i * 512:(i + 1) * 512] for i in range(3)]
    stg_sil = [stg[:, 1536 + i * 512:1536 + (i + 1) * 512] for i in range(3)]
    stg_b = stg[:, 3072:5632].bitcast(BF16)
    stg_sq = [stg_b[:, i * 512:(i + 1) * 512] for i in range(3)]
    stg_d = [stg_b[:, 1536 + i * 512:1536 + (i + 1) * 512] for i in range(4)]
    pt_sb = stg_b[:, 3584:4608].rearrange("p (m t) -> p m t", m=2)
    stg_n = [stg[:, 0:1024], stg[:, 1024:2048]]
    stg_p4 = [smx[:, i * 256:(i + 1) * 256] for i in range(4)]
    smx_b = smx[:, 1024:1536].bitcast(BF16)
    stg_pn4 = [smx_b[:, i * 256:(i + 1) * 256] for i in range(4)]

    ST_SSX = 0
    ST_RSX = 20
    ST_VS = 40
    ST_VQ = 56
    ST_VM = 72
    ST_VR = 88
    ST_VB = 104
    ST_VT = 120
    ST_RBR = 136
    ST_MX = 184
    ST_NB = 192
    ST_RI = 200
    ST_SO = 216
    ST_FO = 280
    ST_RS = 320

    def sc(i):
        return stat[:, i:i + 1]

    ST_EPS = 639

    def rsqrt_cols(out, in_, mult, stb, extra_reads=()):
        P.op("act", lambda e: e.activation(out=out, in_=in_, func=AF.Sqrt, bias=sc(ST_EPS), scale=mult),
             reads=[stb] + list(extra_reads), writes=[stb])
        P.op("dve", lambda e: e.reciprocal(out=out, in_=out), reads=[stb], writes=[stb])

    ps = es.enter_context(nc.psum_tensor("ps", [128, 8, 512], F32))

    B = {}

    def buf(*key):
        if key not in B:
            B[key] = Buf(str(key))
        return B[key]

    bank = [buf("bank", i) for i in range(8)]
    rot = {"i": 0, "p": 0}

    last_use = {b_: 0 for b_ in range(1, 8)}

    def alloc_bank():
        rot["i"] += 1
        b = min(range(1, 8), key=lambda b_: last_use[b_])
        last_use[b] = rot["i"]
        return b

    def alloc_pair():
        rot["i"] += 1
        b = min((1, 3, 5), key=lambda b_: max(last_use[b_], last_use[b_ + 1]))
        last_use[b] = rot["i"]
        last_use[b + 1] = rot["i"]
        return b

    def mm(out, lhsT, rhs, start, stop):
        return lambda e: e.matmul(out, lhsT=lhsT, rhs=rhs, start=start, stop=stop)

    def tr(out, in_):
        return lambda e: e.transpose(out, in_, ident[:])

    def psb(b, lo=0, hi=512):
        return ps[:, b, lo:hi]

    def psb_bf(b):
        return ps[:, b, :].bitcast(BF16)

    def dma_sp(out, in_, reads=(), writes=(), sem=None):
        return P.op("sp", lambda e: e.dma_start(out=out, in_=in_), reads, writes, dma_sem=sem)

    def dma_pool(out, in_, reads=(), writes=(), sem=None):
        return P.op("pool", lambda e: e.dma_start(out=out, in_=in_), reads, writes, dma_sem=sem)

    def cload(q, out, in_):
        if q == "sp":
            dma_sp(out, in_, sem="d_const")
        else:
            dma_pool(out, in_, sem="d_constp")

    cload("sp", ident[:], ident_d)
    cload("sp", trilm[:], tril_d)
    cload("sp", mcur[:], mcur_d)
    cload("sp", mprev[:], mprev_d)
    cload("sp", mc0h[:], mc0h_d)
    cload("sp", mc0l[:], mc0l_d)
    cload("sp", mp0[:], mp0_d)
    cload("sp", pscale[:], pool_scale_d.rearrange("(j p) -> p j", p=128))
    cload("sp", gbr[:], bn_d.rearrange("(j p) -> p j", p=128))
    cload("pool", poolw[:].rearrange("p g c d -> p (g c) d"),
          pool_w_d.rearrange("g (cc p) d -> p (g cc) d", p=128))
    P.op("dve", lambda e: e.memset(ones_bf[:], 1.0))
    P.op("dve", lambda e: e.memset(stat[:], 0.0), writes=[buf("stat_init")])
    P.op("dve", lambda e: e.memset(sc(ST_EPS), EPS), writes=[buf("stat_init")])
    P.barrier(exclude=("sp", "pool"))

    guard = {"yT": [], "pending_ss": [], "hT": [], "fin": [], "xt": []}
    xt_b = [buf("xt", 0), buf("xt", 1)]
    xs_b = [buf("xs", 0), buf("xs", 1)]
    gbc = buf("gbcast")

    def load_gain(src_d):
        P.op("sp", lambda e: e.dma_start(out=gb_v, in_=src_d.partition_broadcast(128)), writes=[gbc],
             waits=guard.get("xt", []), dma_sem="d_g")

    def nt_load(i, src_rows, waits=()):
        P.op("sp", lambda e: e.dma_start(out=xt_v[i % 2], in_=src_rows), writes=[xt_b[i % 2]], waits=waits,
             dma_sem=f"d_xt{i % 2}")

    def nt_stage_a(i, src_rows, col_ss, col_rs):
        xb, xsb = xt_b[i % 2], xs_b[i % 2]
        xt, xs = xt_v[i % 2], xs_v[i % 2]
        stb = buf("st_x", col_ss)
        P.op("act", lambda e: e.activation(out=xs, in_=xt, func=AF.Square, accum_out=sc(col_ss)),
             reads=[xb], writes=[xsb, stb], waits=guard.get("xt", []))
        rsqrt_cols(sc(col_rs), sc(col_ss), 1.0 / D, stb)
        P.op("dve", lambda e: e.scalar_tensor_tensor(out=xs, in0=xt, scalar=sc(col_rs), in1=gb_v,
                                                     op0=ALU.mult, op1=ALU.mult),
             reads=[xb, gbc, stb], writes=[xsb])

    def nt_stage_b(i, dests, dest_bufs):
        xsb = xs_b[i % 2]
        xs = xs_v[i % 2]
        gw = guard.get("hT", []) if guard is not None else []
        for half in range(2):
            b = alloc_bank()
            pv = psb_bf(b)
            fns = [tr(pv[:, j * 128:(j + 1) * 128], xs[:, (half * 8 + j) * 128:(half * 8 + j + 1) * 128])
                   for j in range(8)]
            P.group("pe", fns, reads=[xsb], writes=[bank[b]])
            src = pv.rearrange("p (k t) -> p k t", k=8)
            if half == 0:
                P.op("act", lambda e, s=src, d=dests[0]: e.copy(out=d, in_=s),
                     reads=[bank[b]], writes=[dest_bufs[0]], waits=gw)
            else:
                P.op("dve", lambda e, s=src, d=dests[1]: e.tensor_copy(out=d, in_=s),
                     reads=[bank[b]], writes=[dest_bufs[1]], waits=gw)

    def norm_transpose_many(items, gain_src, late_waits=(), after_b=None):
        n = len(items)
        nt_load(0, items[0][0], waits=guard.get("xt", []))
        if n > 1:
            nt_load(1, items[1][0], waits=guard.get("xt", []))
        load_gain(gain_src)
        nt_stage_a(0, *items[0][:3])
        for i in range(n):
            if i + 1 < n:
                nt_stage_a(i + 1, *items[i + 1][:3])
            if i + 2 < n:
                nt_load(i + 2, items[i + 2][0], waits=late_waits)
            nt_stage_b(i, *items[i][3:])
            if after_b is not None:
                after_b(i)

    memnT = memnT_t[:]
    gpost_v = memnT_t[:].rearrange("p k m -> p (k m)").bitcast(F32)
    gpost_b = buf("gpost")
    norm_transpose_many([
        (mem_d[mt * 128:(mt + 1) * 128, :], ST_SSX + 17 + mt, ST_RSX + 17 + mt,
         [memnT[:, 0:8, mt * 128:(mt + 1) * 128], memnT[:, 8:16, mt * 128:(mt + 1) * 128]],
         [buf("memnT", mt, 0), buf("memnT", mt, 1)]) for mt in range(2)], mem_norm_d)
    memnT_bs = [buf("memnT", mt, hf) for mt in range(2) for hf in range(2)]

    wslot_b = [buf("wslot", i) for i in range(6)]

    def w_view(src_d, c0):
        return src_d[:, c0:c0 + 256].rearrange("(k p) c -> p k c", p=128)

    def wo_view(db, br):
        return w_out_d[br * 1024:(br + 1) * 1024, db * 512:(db + 1) * 512].rearrange("(k p) c -> p k c", p=128)

    stream = []
    for p_ in range(NPASS):
        for i in range(4):
            stream.append(((p_, "v", i), 0, w_view(w_in_d, C_V + 256 * i)))
        for j in range(4):
            stream.append(((p_, "u", j), 0, w_view(w_in_d, C_U + 256 * j)))
            stream.append(((p_, "gb", j), 0, w_view(w_in_d, C_GB + 256 * j)))
        for g in range(4):
            stream.append(((p_, "xa", g), 0, w_view(w_in_d, C_XA + 256 * g)))
            stream.append(((p_, "ga", g), 0, w_view(w_in_d, C_GA + 256 * g)))
        if p_ == 0:
            for j in range(8):
                stream.append((("kv", j), 0, w_view(w_kv_d, j * 256)))
        for hh in range(4):
            stream.append(((p_, "q", hh), 0, w_view(w_in_d, C_Q + 256 * hh)))
            stream.append(((p_, "gc", hh), 0, w_view(w_in_d, C_GC + 256 * hh)))
        for db in range(4):
            for br in range(3):
                stream.append(((p_, "wo", db, br), 1, wo_view(db, br)))
    skey = {k: i for i, (k, _, _) in enumerate(stream)}
    sstate = {"next": 0}
    LOOKAHEAD = 4

    def wslot16(i):
        return wpool[:, i * 4096:(i + 1) * 4096].rearrange("p (k c) -> p k c", k=16)

    def wslot8(i):
        return wpool[:, i * 4096:(i + 1) * 4096].rearrange("p (k c) -> p k c", k=8)

    def need(*keys):
        last = min(max(skey[k] for k in keys) + LOOKAHEAD, min(skey[k] for k in keys) + 5)
        while sstate["next"] <= last and sstate["next"] < len(stream):
            i = sstate["next"]
            k, kind, view = stream[i]
            s = i % 6
            dst = wslot16(s) if kind == 0 else wslot8(s)
            dma_pool(dst, view, writes=[wslot_b[s]], sem=f"d_w{s}")
            sstate["next"] += 1
        return [skey[k] % 6 for k in keys]

    wtmp_f = stg[:, 0:1024].rearrange("p (h s) -> p h s", h=8)
    wtmp_b = stg_b[:, 0:1024].rearrange("p (h s) -> p h s", h=8)
    wt_b = buf("wtmp")
    wtb_b = buf("wtmpb")
    wsT_b = buf("wsT")
    dma_sp(wtmp_f, sgu_w_d.rearrange("h t s -> t h s"), writes=[wt_b], sem="d_misc")
    for h in range(8):
        P.op("dve", lambda e, h=h: e.tensor_tensor(out=wtmp_b[:, h, :], in0=wtmp_f[:, h, :], in1=trilm[:],
                                                    op=ALU.mult), reads=[wt_b], writes=[wtb_b])
    b = alloc_bank()
    pv = psb_bf(b)
    P.group("pe", [tr(pv[:, h * 128:(h + 1) * 128], wtmp_b[:, h, :]) for h in range(8)],
            reads=[wtb_b], writes=[bank[b]])
    P.op("act", lambda e, pv=pv: e.copy(out=wsT[:].rearrange("p h t -> p (h t)"), in_=pv),
         reads=[bank[b]], writes=[wsT_b])

    kv_b = buf("kv")
    need((0, "v", 0))

    hTh_b = buf("hTh")
    hT_bs = [[buf("hT", tt, hf) for hf in range(2)] for tt in range(NTT)]
    hT_all = [b_ for l in hT_bs for b_ in l]
    yT_b = [[buf("yT", c, tb) for tb in range(2)] for c in range(24)]
    vn_b = [buf("vn", t) for t in range(8)]
    ssb = bank[0]
    tmp_b = [buf("tmp", i) for i in range(3)]
    sil_b = [buf("sil", i) for i in range(3)]
    sq_b = [buf("sq", i) for i in range(3)]
    d_b = [buf("dstg", i) for i in range(4)]
    rr = {"y": 0, "d": 0, "c": 0, "sm": 0, "rs": 0}
    orow_b = [buf("orow", t) for t in range(8)]
    xa_save_b = buf("xa_save")
    lnc_b = buf("lnconst")
    pt_b = buf("pt_sb")
    xa_tm = vnr[:, 0:2304].rearrange("p (t c) -> p t c", t=9)
    xa_b = [buf("xa_tm", t) for t in range(9)]
    ENG_SEMS = ("e_pe", "e_act", "e_dve")

    def out_row(tt):
        if tt < 4:
            return hT[:].bitcast(F32)[:, tt * 2048:(tt + 1) * 2048]
        if tt < 6:
            return vnr[:].bitcast(F32)[:, (tt - 4) * 2048:(tt - 3) * 2048]
        return stg[:, (tt - 6) * 2048:(tt - 5) * 2048]

    def flush_ss(keep=0):
        pend = guard["pending_ss"]
        n = max(0, len(pend) - keep)
        for f in pend[:n]:
            f()
        guard["pending_ss"] = pend[n:]

    def finish_y(p, br, chunk_in_br, tb, pre_fn, pre_reads):
        flush_ss(keep=1)
        i = rr["y"] % 3
        rr["y"] += 1
        c = br * 8 + chunk_in_br
        tmp = stg_tmp[i]
        P.op("dve", lambda e: pre_fn(e, tmp), reads=pre_reads, writes=[tmp_b[i]])
        P.op("act", lambda e: e.activation(out=yT3[:, c, tb * 512:(tb + 1) * 512], in_=tmp, func=AF.Copy,
                                           scale=gbr[:, c:c + 1]),
             reads=[tmp_b[i]], writes=[yT_b[c][tb]], waits=guard["yT"])
        sq = stg_sq[i]
        P.op("act", lambda e: e.activation(out=sq, in_=tmp, func=AF.Square), reads=[tmp_b[i]], writes=[sq_b[i]])

        def ss_group():
            fns = []
            for q in range(4):
                col = br * 8 + tb * 4 + q
                fns.append(lambda e, col=col, q=q: e.matmul(ps[:, 0, col:col + 1],
                                                            lhsT=sq[:, q * 128:(q + 1) * 128],
                                                            rhs=ones_bf[:, 0:1], start=False,
                                                            stop=(chunk_in_br == 7), skip_group_check=True))
            P.group("pe", fns, reads=[sq_b[i]], writes=[ssb])
        guard["pending_ss"].append(ss_group)

    def branch_rstd(p, br):
        pass

    def all_branch_rstd(p):
        flush_ss()
        c0 = ST_RBR + p * 24
        rsqrt_cols(stat[:, c0:c0 + 24], ps[:, 0, 0:24], 1.0 / 1024, buf("st_rbr", p), extra_reads=[ssb])

    def silu_of(b):
        i = rr["c"] % 3
        rr["c"] += 1
        P.op("act", lambda e: e.activation(out=stg_sil[i], in_=psb(b), func=AF.Silu),
             reads=[bank[b]], writes=[sil_b[i]])
        return i

    def proj_fm(slot, col0, tb, b):
        W = wslot16(slot)
        fns = [mm(psb(b), W[:, k, col0:col0 + 128], hT3[:, k, tb * 512:(tb + 1) * 512], k == 0, k == 15)
               for k in range(16)]
        P.group("pe", fns, reads=[wslot_b[slot]] + hT_all[tb * 8:(tb + 1) * 8], writes=[bank[b]])

    def copy_alt(k, out, in_, reads, writes, waits=()):
        if k % 2 == 0:
            P.op("act", lambda e: e.copy(out=out, in_=in_), reads=reads, writes=writes, waits=waits)
        else:
            P.op("dve", lambda e: e.tensor_copy(out=out, in_=in_), reads=reads, writes=writes, waits=waits)

    def emit_kv():
        for j in range(8):
            (s,) = need(("kv", j))
            W = wslot16(s)
            if j < 4:
                for q in range(2):
                    b = alloc_bank()
                    fns = [mm(psb(b, 0, 256), W[:, k, q * 128:(q + 1) * 128], memnT[:, k, :], k == 0, k == 15)
                           for k in range(16)]
                    P.group("pe", fns, reads=[wslot_b[s]] + memnT_bs, writes=[bank[b]])
                    P.op("act", lambda e, b=b, c=j * 2 + q: e.copy(out=kT[:, c, :], in_=psb(b, 0, 256)),
                         reads=[bank[b]], writes=[kv_b])
            else:
                for mt in range(2):
                    b = alloc_bank()
                    fns = [mm(psb(b, 0, 256), memnT[:, k, mt * 128:(mt + 1) * 128], W[:, k, :], k == 0, k == 15)
                           for k in range(16)]
                    P.group("pe", fns, reads=[wslot_b[s]] + memnT_bs, writes=[bank[b]])
                    P.op("dve", lambda e, b=b, mt=mt, c0=(j - 4) * 256: e.tensor_copy(
                        out=vmem[:, mt, c0:c0 + 256], in_=psb(b, 0, 256)), reads=[bank[b]], writes=[kv_b])


    for p in range(NPASS):
        P.op("dve", lambda e: e.memset(ps[:, 0, 0:32], 0.0), writes=[ssb])
        items = []
        for tt in range(NTT):
            gt = p * NTT + tt
            items.append((x_d[gt * 128:(gt + 1) * 128, :], ST_SSX + gt, ST_RSX + gt,
                          [hT3[:, 0:8, tt * 128:(tt + 1) * 128], hT3[:, 8:16, tt * 128:(tt + 1) * 128]],
                          hT_bs[tt]))
        if p == 0:
            items.append((xh_d, ST_SSX + 16, ST_RSX + 16,
                          [hT_halo[:, 0:8, :], hT_halo[:, 8:16, :]], [hTh_b, hTh_b]))
        vstate = {"started": False, "done": 0}

        def pre_v(p=p):
            if guard["fin"]:
                for e_ in ("pe", "act", "dve", "sp"):
                    P.lists[e_].append((None, sorted(set(guard["fin"])), None))
                guard["fin"] = []
            dma_sp(lng_v, ln_g_d.partition_broadcast(128), writes=[lnc_b], sem="d_misc")
            dma_sp(lnb_v, ln_b_d.partition_broadcast(128), writes=[lnc_b], sem="d_misc2")
            dma_sp(bsb_v.rearrange("p h t -> p (h t)"), sgu_b_d.partition_broadcast(128), writes=[lnc_b],
                   sem="d_misc3")
            vstate["vs"] = need(*[(p, "v", i) for i in range(4)])
            vstate["started"] = True

        def v_tile(tt, p=p):
            if not vstate["started"]:
                pre_v()
            vs = vstate["vs"]
            gt = p * NTT + tt
            b = alloc_pair()
            for i4 in range(4):
                s = vs[i4]
                W = wslot16(s)
                bb = b + i4 // 2
                o = psb(bb, (i4 % 2) * 256, (i4 % 2) * 256 + 256)
                fns = [mm(o, hT3[:, k, tt * 128:(tt + 1) * 128], W[:, k, :], k == 0, k == 15)
                       for k in range(16)]
                P.group("pe", fns, reads=[wslot_b[s]] + hT_bs[tt], writes=[bank[bb]])
            pv = ps[:, b:b + 2, :].rearrange("p a c -> p (a c)")
            nb_ = stg_n[tt % 2]
            nbuf = [tmp_b[0], tmp_b[1]] if tt % 2 == 0 else [tmp_b[2], sil_b[0]]
            stb = buf("st_ln", gt)
            P.op("act", lambda e, pv=pv, gt=gt: e.activation(out=junk[:], in_=pv, func=AF.Identity,
                                                             accum_out=sc(ST_VS + gt)),
                 reads=[bank[b], bank[b + 1]], writes=[stb, buf("junk")])
            P.op("act", lambda e, pv=pv, gt=gt: e.activation(out=junk[:], in_=pv, func=AF.Square,
                                                             accum_out=sc(ST_VQ + gt)),
                 reads=[bank[b], bank[b + 1]], writes=[stb, buf("junk")])
            P.op("dve", lambda e, gt=gt: e.tensor_scalar(out=sc(ST_VM + gt), in0=sc(ST_VS + gt),
                                                         scalar1=1.0 / 1024, scalar2=None, op0=ALU.mult),
                 reads=[stb], writes=[stb])
            P.op("dve", lambda e, gt=gt: e.tensor_tensor(out=sc(ST_VT + gt), in0=sc(ST_VM + gt),
                                                         in1=sc(ST_VM + gt), op=ALU.mult),
                 reads=[stb], writes=[stb])
            P.op("dve", lambda e, gt=gt: e.scalar_tensor_tensor(out=sc(ST_VR + gt), in0=sc(ST_VQ + gt),
                                                                scalar=1.0 / 1024, in1=sc(ST_VT + gt),
                                                                op0=ALU.mult, op1=ALU.subtract),
                 reads=[stb], writes=[stb])
            rsqrt_cols(sc(ST_VR + gt), sc(ST_VR + gt), 1.0, stb)
            P.op("dve", lambda e, gt=gt: e.scalar_tensor_tensor(out=sc(ST_VB + gt), in0=sc(ST_VM + gt),
                                                                scalar=-1.0, in1=sc(ST_VR + gt),
                                                                op0=ALU.mult, op1=ALU.mult),
                 reads=[stb], writes=[stb])
            P.op("act", lambda e, pv=pv, gt=gt, nb_=nb_: e.activation(out=nb_, in_=pv, func=AF.Identity,
                                                                      bias=sc(ST_VB + gt), scale=sc(ST_VR + gt)),
                 reads=[bank[b], bank[b + 1], stb], writes=nbuf)
            P.op("dve", lambda e, nb_=nb_: e.tensor_tensor(out=nb_, in0=nb_, in1=lng_v, op=ALU.mult),
                 reads=[lnc_b] + nbuf, writes=nbuf)
            P.op("dve", lambda e, nb_=nb_, tt=tt: e.tensor_tensor(out=vn3[:, tt, :], in0=nb_, in1=lnb_v,
                                                                  op=ALU.add),
                 reads=[lnc_b] + nbuf, writes=[vn_b[tt]])
            vstate["done"] = tt + 1

        def after_b(i):
            if 1 <= i <= NTT:
                v_tile(i - 1)

        norm_transpose_many(items, norm_pre_d, after_b=after_b)
        for tt in range(vstate["done"], NTT):
            v_tile(tt)
        guard["hT"] = []
        guard["yT"] = P.snapshot(ENG_SEMS)

        for j in range(4):
            su, sg_ = need((p, "u", j), (p, "gb", j))
            for hl in range(2):
                h = 2 * j + hl
                for tb in range(2):
                    bu, bg, bz = alloc_bank(), alloc_bank(), alloc_bank()
                    proj_fm(sg_, hl * 128, tb, bg)
                    proj_fm(su, hl * 128, tb, bu)
                    fns = [mm(psb(bz, q * 128, (q + 1) * 128), vn3[:, tb * 4 + q, h * 128:(h + 1) * 128],
                              wsT[:, h, :], True, True) for q in range(4)]
                    P.group("pe", fns, reads=[vn_b[tb * 4 + q] for q in range(4)] + [wsT_b], writes=[bank[bz]])
                    si = silu_of(bg)
                    P.op("dve", lambda e, si=si, bu=bu: e.tensor_tensor(out=stg_sil[si], in0=psb(bu),
                                                                        in1=stg_sil[si], op=ALU.mult),
                         reads=[bank[bu], sil_b[si]], writes=[sil_b[si]])

                    def preB(e, tmp, bz=bz, h=h):
                        return e.tensor_tensor(out=tmp.rearrange("p (q t) -> p q t", q=4),
                                               in0=psb(bz).rearrange("p (q t) -> p q t", q=4),
                                               in1=bsb_v[:, h:h + 1, :].to_broadcast([128, 4, 128]), op=ALU.add)
                    flush_ss(keep=1)
                    i = rr["y"] % 3
                    P.op("dve", lambda e, preB=preB, i=i: preB(e, stg_tmp[i]), reads=[bank[bz], lnc_b],
                         writes=[tmp_b[i]])

                    def pre2(e, tmp, si=si):
                        return e.tensor_tensor(out=tmp, in0=tmp, in1=stg_sil[si], op=ALU.mult)
                    finish_y(p, 1, h, tb, pre2, [sil_b[si], tmp_b[i]])
        branch_rstd(p, 1)

        xcp_tok = dma_sp(out_d[p * TP:(p + 1) * TP, :], x_d[p * TP:(p + 1) * TP, :], sem="d_xcp")
        for g in range(4):
            sx, sga = need((p, "xa", g), (p, "ga", g))
            W = wslot16(sx)
            if p == 0:
                b = alloc_bank()
                fns = [mm(psb(b, 0, 256), hT_halo[:, k, :], W[:, k, :], k == 0, k == 15) for k in range(16)]
                P.group("pe", fns, reads=[wslot_b[sx], hTh_b], writes=[bank[b]])
                P.op("act", lambda e, b=b: e.copy(out=xa_tm[:, 0, :], in_=psb(b, 0, 256)), reads=[bank[b]],
                     writes=[xa_b[0]] + vn_b)
            else:
                P.op("act", lambda e, g=g: e.copy(out=xa_tm[:, 0, :], in_=xa_save[:, g * 256:(g + 1) * 256]),
                     reads=[xa_save_b], writes=[xa_b[0]] + vn_b)
            for tt in range(NTT):
                b = alloc_bank()
                o = psb(b, 0, 256)
                fns = [mm(o, hT3[:, k, tt * 128:(tt + 1) * 128], W[:, k, :], k == 0, k == 15)
                       for k in range(16)]
                P.group("pe", fns, reads=[wslot_b[sx]] + hT_bs[tt], writes=[bank[b]])
                copy_alt(tt, xa_tm[:, tt + 1, :], o, [bank[b]], [xa_b[tt + 1]] + vn_b)
            if p == 0:
                P.op("act", lambda e, g=g: e.copy(out=xa_save[:, g * 256:(g + 1) * 256],
                                                  in_=xa_tm[:, 8, :]),
                     reads=[xa_b[8]], writes=[xa_save_b])
            for tb in range(2):
                dsl = []
                for cc in range(2):
                    cols = slice(cc * 128, cc * 128 + 128)
                    b = alloc_bank()
                    fns = []
                    rd = []
                    for q in range(4):
                        tt = tb * 4 + q
                        o = psb(b, q * 128, (q + 1) * 128)
                        if p == 0 and tt == 0:
                            fns += [mm(o, xa_tm[:, 1, cols], mc0h[:, g, :], True, False),
                                    mm(o, xa_tm[:, 1, cols], mc0l[:, g, :], False, False),
                                    mm(o, xa_tm[:, 0, cols], mp0[:, g, :], False, True)]
                        else:
                            fns += [mm(o, xa_tm[:, tt + 1, cols], mcur[:, g, :], True, False),
                                    mm(o, xa_tm[:, tt, cols], mprev[:, g, :], False, True)]
                        rd += [xa_b[tt], xa_b[tt + 1]]
                    P.group("pe", fns, reads=rd, writes=[bank[b]])
                    di = rr["d"] % 4
                    rr["d"] += 1
                    copy_alt(cc, stg_d[di], psb(b), [bank[b]], [d_b[di]])
                    dsl.append(di)
                bgs = []
                for dc in range(2):
                    bg = alloc_bank()
                    proj_fm(sga, dc * 128, tb, bg)
                    bgs.append(bg)
                sis = [silu_of(bg) for bg in bgs]
                for dc in range(2):
                    by = alloc_bank()
                    fns = [mm(psb(by), poolw[:, g, cc, dc * 128:(dc + 1) * 128], stg_d[dsl[cc]], cc == 0, cc == 1)
                           for cc in range(2)]
                    P.group("pe", fns, reads=[d_b[dsl[0]], d_b[dsl[1]]], writes=[bank[by]])
                    si = sis[dc]
                    ch = g * 2 + dc

                    def preA(e, tmp, by=by, si=si, ch=ch):
                        return e.scalar_tensor_tensor(out=tmp, in0=psb(by), scalar=pscale[:, ch:ch + 1],
                                                      in1=stg_sil[si], op0=ALU.mult, op1=ALU.mult)
                    finish_y(p, 0, ch, tb, preA, [bank[by], sil_b[si]])
        branch_rstd(p, 0)

        blocks = [(hh, tb) for hh in range(4) for tb in range(2)]
        cst = {}

        def c_stage1a(n):
            hh, tb = blocks[n]
            sq_, sgc = need((p, "q", hh), (p, "gc", hh))
            qsl = []
            for dc in range(2):
                b = alloc_bank()
                proj_fm(sq_, dc * 128, tb, b)
                di = rr["d"] % 4
                rr["d"] += 1
                copy_alt(dc, stg_d[di], psb(b), [bank[b]], [d_b[di]])
                qsl.append(di)
            cst[n] = {"qsl": qsl, "sgc": sgc}

        def c_stage1b(n):
            hh, tb = blocks[n]
            qsl = cst[n]["qsl"]
            bp = alloc_pair()
            k4 = (rr["sm"] % 2) * 4
            rr["sm"] += 1
            stb = buf("st_sm", k4)
            rs0 = ST_RS + rr["rs"]
            rr["rs"] += 4
            sos = []
            for q in range(4):
                bs_ = bp + q // 2
                so = psb(bs_, (q % 2) * 256, (q % 2) * 256 + 256)
                fns = [mm(so, stg_d[qsl[dc]][:, q * 128:(q + 1) * 128], kT[:, hh * 2 + dc, :],
                          dc == 0, dc == 1) for dc in range(2)]
                P.group("pe", fns, reads=[d_b[qsl[0]], d_b[qsl[1]], kv_b], writes=[bank[bs_]])
                sos.append(so)
            pv4 = ps[:, bp:bp + 2, :].rearrange("p a (h m) -> p (a h) m", h=2)
            P.op("dve", lambda e, pv4=pv4, k4=k4: e.reduce_max(out=stat[:, ST_MX + k4:ST_MX + k4 + 4], in_=pv4,
                                                                axis=AX.X),
                 reads=[bank[bp], bank[bp + 1]], writes=[stb])
            P.op("dve", lambda e, k4=k4: e.tensor_scalar(out=stat[:, ST_NB + k4:ST_NB + k4 + 4],
                                                        in0=stat[:, ST_MX + k4:ST_MX + k4 + 4],
                                                        scalar1=-1.0 / 16, scalar2=None, op0=ALU.mult),
                 reads=[stb], writes=[stb])
            pbs = [buf("smP", q) for q in range(4)]
            for q in range(4):
                P.op("act", lambda e, so=sos[q], q=q, k4=k4, rs0=rs0: e.activation(
                    out=stg_p4[q], in_=so, func=AF.Exp, bias=sc(ST_NB + k4 + q), scale=1.0 / 16,
                    accum_out=sc(rs0 + q)), reads=[bank[bp + q // 2], stb], writes=[pbs[q], buf("st_rs", rs0 + q)])
            P.op("dve", lambda e, k4=k4, rs0=rs0: e.reciprocal(out=stat[:, ST_RI + k4:ST_RI + k4 + 4],
                                                              in_=stat[:, rs0:rs0 + 4]),
                 reads=[buf("st_rs", rs0 + q) for q in range(4)], writes=[stb])
            pns = []
            for q in range(4):
                pnb = buf("smPn", q)
                P.op("dve", lambda e, q=q, k4=k4: e.tensor_scalar(out=stg_pn4[q], in0=stg_p4[q],
                                                                 scalar1=sc(ST_RI + k4 + q), scalar2=None,
                                                                 op0=ALU.mult),
                     reads=[pbs[q], stb], writes=[pnb])
                pns.append(q)
            cst[n]["pns"] = pns

        def c_stage2(n):
            hh, tb = blocks[n]
            st_ = cst[n]
            bgs = []
            for dc in range(2):
                bg = alloc_bank()
                proj_fm(st_["sgc"], dc * 128, tb, bg)
                bgs.append(bg)
            bt = alloc_bank()
            ptv = psb_bf(bt).rearrange("p (m t) -> p m t", m=2)
            for q in range(4):
                pi = st_["pns"][q]
                fns = [tr(ptv[:, mc, q * 128:(q + 1) * 128], stg_pn4[pi][:, mc * 128:(mc + 1) * 128])
                       for mc in range(2)]
                P.group("pe", fns, reads=[buf("smPn", pi)], writes=[bank[bt]])
            P.op("act", lambda e, bt=bt: e.copy(out=pt_sb.rearrange("p m t -> p (m t)"), in_=psb_bf(bt)),
                 reads=[bank[bt]], writes=[pt_b])
            st_["sis"] = [silu_of(bg) for bg in bgs]

        def c_stage3(n):
            hh, tb = blocks[n]
            st_ = cst[n]
            for dc in range(2):
                bo = alloc_bank()
                c0 = hh * 256 + dc * 128
                fns = [mm(psb(bo), vmem[:, mc, c0:c0 + 128], pt_sb[:, mc, :], mc == 0, mc == 1)
                       for mc in range(2)]
                P.group("pe", fns, reads=[pt_b, kv_b], writes=[bank[bo]])
                si = st_["sis"][dc]

                def preC(e, tmp, bo=bo, si=si):
                    return e.tensor_tensor(out=tmp, in0=psb(bo), in1=stg_sil[si], op=ALU.mult)
                finish_y(p, 2, hh * 2 + dc, tb, preC, [bank[bo], sil_b[si]])

        if p == 0:
            emit_kv()
            dma_sp(gpost_v, norm_post_d.partition_broadcast(128), writes=[gpost_b] + memnT_bs, sem="d_gpost")
        c_stage1a(0)
        c_stage1b(0)
        for n in range(len(blocks)):
            c_stage2(n)
            if n + 1 < len(blocks):
                c_stage1a(n + 1)
            c_stage3(n)
            if n + 1 < len(blocks):
                c_stage1b(n + 1)
        branch_rstd(p, 2)

        g_out = P.snapshot(ENG_SEMS)
        out_toks = []

        def final_chain(tt, p=p):
            gt = p * NTT + tt
            stf = buf("st_fo", p, tt)
            c0 = ST_SO + p * 32 + tt * 4
            f = ST_FO + p * 8 + tt
            P.op("dve", lambda e: e.reduce_sum(out=sc(f), in_=stat[:, c0:c0 + 4], axis=AX.X),
                 reads=[buf("st_so", p, tt)], writes=[stf])
            rsqrt_cols(sc(f + 16), sc(f), 1.0 / D, stf)
            orow = out_row(tt)
            P.op("dve", lambda e: e.scalar_tensor_tensor(out=orow, in0=orow, scalar=sc(f + 16), in1=gpost_v,
                                                         op0=ALU.mult, op1=ALU.mult),
                 reads=[stf, gpost_b, orow_b[tt]], writes=[orow_b[tt]])
            out_toks.append(P.op(
                "pool", lambda e: e.dma_start(out=out_d[gt * 128:(gt + 1) * 128, :], in_=orow, accum_op=ALU.add),
                reads=[orow_b[tt]], waits=[xcp_tok], dma_sem=f"d_out{tt}"))

        for db in range(4):
            sl3 = need(*[(p, "wo", db, br) for br in range(3)])
            for tt in range(NTT):
                orow = out_row(tt)[:, db * 512:(db + 1) * 512]
                bks = []
                for br in range(3):
                    b = alloc_bank()
                    Wo = wslot8(sl3[br])
                    fns = [mm(psb(b), yT3[:, br * 8 + q, tt * 128:(tt + 1) * 128], Wo[:, q, :], q == 0, q == 7)
                           for q in range(8)]
                    rd = [yT_b[br * 8 + q][tt // 4] for q in range(8)] + [wslot_b[sl3[br]]]
                    P.group("pe", fns, reads=rd, writes=[bank[b]])
                    bks.append(b)
                if db == 0 and tt == 0:
                    all_branch_rstd(p)
                rc = ST_RBR + p * 24
                stbs = [buf("st_rbr", p) for br in range(3)]
                P.op("act", lambda e, b=bks[0], orow=orow, c=rc + tt: e.activation(
                    out=orow, in_=psb(b), func=AF.Identity, scale=sc(c)),
                    reads=[bank[bks[0]], stbs[0]], writes=[orow_b[tt]], waits=g_out)
                for br in (1, 2):
                    P.op("dve", lambda e, b=bks[br], orow=orow, c=rc + br * 8 + tt: e.scalar_tensor_tensor(
                        out=orow, in0=psb(b), scalar=sc(c), in1=orow, op0=ALU.mult, op1=ALU.add),
                        reads=[bank[bks[br]], stbs[br], orow_b[tt]], writes=[orow_b[tt]], waits=g_out)
                P.op("act", lambda e, orow=orow, c=ST_SO + p * 32 + tt * 4 + db: e.activation(
                    out=junk[:, 0:512], in_=orow, func=AF.Square, accum_out=sc(c)),
                    reads=[orow_b[tt]], writes=[buf("st_so", p, tt), buf("junk")])
                if db == 3 and tt >= 1:
                    final_chain(tt - 1)
        final_chain(NTT - 1)
        if p + 1 < NPASS:
            need((p + 1, "v", 0))

        if debug and p == NPASS - 1:
            dbg_y = nc.dram_tensor("dbg_y", [128, 24 * TP], BF16, kind="ExternalOutput").ap()
            dbg_s = nc.dram_tensor("dbg_s", [128, 640], F32, kind="ExternalOutput").ap()
            P.barrier(exclude=("pool",))
            dma_sp(dbg_y, yT[:], sem="d_dbg")
            dma_sp(dbg_s, stat[:], sem="d_dbg")
            P.barrier(exclude=("pool",))
        guard["xt"] = P.snapshot(("e_pe",))
        guard["hT"] = out_toks[0:4]
        guard["fin"] = out_toks + P.snapshot(ENG_SEMS)
        if p + 1 == NPASS:
            P.barrier()

    sem_names = sorted(P.count.keys())
    sems = {n: es.enter_context(nc.semaphore(n)) for n in sem_names}
    with nc.allow_non_contiguous_dma(reason="tiny per-partition parameter vectors"):
        with nc.Block() as block:
            P.emit(nc, block, sems)
    es.close()
    return nc


_CACHE = {}


def _consts(half):
    cur, prev, cur0 = _band_constants()
    bf = ml_dtypes.bfloat16

    def lay(a):
        return np.ascontiguousarray(np.transpose(a, (1, 0, 2)))
    c = {}
    c["c_ident"] = np.eye(128, dtype=np.float32).astype(bf)
    t = np.arange(128)[:, None]
    s = np.arange(128)[None, :]
    c["c_tril"] = (s <= t).astype(np.float32)
    c["c_mcur"] = lay(cur).astype(np.float32).astype(bf)
    c["c_mprev"] = lay(prev).astype(np.float32).astype(bf)
    if half == 0:
        hi, lo = _bf16_hi_lo(lay(cur0))
        c["c_mcur0h"], c["c_mcur0l"] = hi, lo
        c["c_mprev0"] = np.zeros((128, 4, 128), bf)
    else:
        c["c_mcur0h"] = c["c_mcur"]
        c["c_mcur0l"] = np.zeros((128, 4, 128), bf)
        c["c_mprev0"] = c["c_mprev"]
    return c


def kernel(x, mem, norm_pre, w_in, pool_w, pool_scale, sgu_ln_g, sgu_ln_b, sgu_w, sgu_b,
           mem_norm, w_kv, branch_norm, w_out, norm_post):
    f = lambda a: np.ascontiguousarray(np.asarray(a), dtype=np.float32)
    x, mem = f(x), f(mem)
    if "nc" not in _CACHE:
        _CACHE["nc"] = build_program()
    nc = _CACHE["nc"]
    shared = {
        "w_in": f(w_in)[0], "w_kv": f(w_kv)[0], "w_out": f(w_out)[0],
        "norm_pre": f(norm_pre)[0].reshape(D), "mem_norm": f(mem_norm).reshape(D),
        "norm_post": f(norm_post)[0].reshape(D),
        "pool_w": f(pool_w)[0], "pool_scale": f(pool_scale)[0],
        "sgu_ln_g": f(sgu_ln_g)[0].reshape(1024), "sgu_ln_b": f(sgu_ln_b)[0].reshape(1024),
        "sgu_w": f(sgu_w)[0], "sgu_b": f(sgu_b)[0].reshape(1024),
        "branch_norm": f(branch_norm)[0],
    }
    in_maps = []
    for c in range(8):
        b, half = c // 2, c % 2
        m = dict(shared)
        m["x"] = x[b, half * TC:(half + 1) * TC]
        m["xh"] = x[b, TC - 128:TC] if half == 1 else np.zeros((128, D), np.float32)
        m["mem"] = mem[b]
        m.update(_consts(half))
        in_maps.append(m)
    res = run_bass_kernel_spmd(nc, in_maps, core_ids=list(range(8)))
    out = np.empty((4, 4096, D), np.float32)
    for c in range(8):
        b, half = c // 2, c % 2
        out[b, half * TC:(half + 1) * TC] = res.results[c]["out"]
    return out
```
